# Optimizing a Trainium2 kernel written in Bass

```python
import math
import numpy as np
import jax
import jax.numpy as jnp
from jax import lax

D_MODEL = 2048
BATCH = 4
SEQ = 4096
DEPTH = 2

CTX_LEN = 256
GRID_W = 64

MIX = D_MODEL
N_GROUPS = 4
GROUP = MIX // N_GROUPS
CHUNK = 64
EPS = 1e-6

GLA_HEADS = 4
GLA_DV = GROUP // GLA_HEADS
GLA_DK = GLA_DV // 2
GLA_LR = 16
GLA_TAU = 16.0

GDN_HEADS = 4
GDN_DK = GROUP // GDN_HEADS
GDN_DV = GDN_DK
CONV_W = 3

RW_HEAD = 64
RW_HEADS = GROUP // RW_HEAD
RW_DECAY_LR = 32
RW_A_LR = 32
RW_GATE_LR = 96
RW_DECAY_SCALE = math.exp(-0.5)
RW_GN_EPS = 64e-5

FFN = -(-(8 * D_MODEL) // (3 * 256)) * 256

PROJ_WIDTHS = (
    GLA_HEADS * GLA_DK, GLA_HEADS * GLA_DK, GROUP, GROUP, GLA_LR,
    3 * GROUP, GROUP, 2 * GDN_HEADS, 2 * GDN_HEADS,
    GROUP, GROUP, GROUP,
    3 * GROUP, RW_DECAY_LR + RW_A_LR, RW_GATE_LR,
)
PROJ = sum(PROJ_WIDTHS)

kernel_name = 'hybrid_parallel_groups_flow_block'


def _rmsnorm(x, g):
    xf = x.astype(jnp.float32)
    y = xf * lax.rsqrt(jnp.mean(xf * xf, axis=-1, keepdims=True) + EPS)
    return (y * g.astype(jnp.float32)).astype(x.dtype)


def _rms_heads(o, g):
    return o * lax.rsqrt(jnp.mean(o * o, axis=-1, keepdims=True) + EPS) * g


def _group_norm(y, w, bias):
    mu = jnp.mean(y, axis=-1, keepdims=True)
    var = jnp.mean(jnp.square(y - mu), axis=-1, keepdims=True)
    yn = (y - mu) * lax.rsqrt(var + RW_GN_EPS)
    return yn.reshape(y.shape[:-2] + (-1,)) * w + bias


def _l2norm(x):
    return x * lax.rsqrt(jnp.sum(x * x, axis=-1, keepdims=True) + EPS)


def _heads(t, n_heads):
    return t.reshape(t.shape[:-1] + (n_heads, t.shape[-1] // n_heads))


def _flip(t):
    return jnp.flip(t, axis=1)


def _to_chunks(t):
    return t.reshape((t.shape[0], t.shape[1] // CHUNK, CHUNK) + t.shape[2:])


def _conv3_rows(x, w, row_len):
    b, t, ch = x.shape
    rows = t // row_len
    xp = jnp.pad(x.reshape(b, rows, row_len, ch), ((0, 0), (0, 0), (1, 1), (0, 0)))
    y = xp[:, :, :-2] * w[0] + xp[:, :, 1:-1] * w[1] + xp[:, :, 2:] * w[2]
    return y.reshape(b, t, ch)


def _token_shift(x, mu):
    prev = jnp.pad(x, ((0, 0), (1, 0), (0, 0)))[:, :-1]
    return x + (prev - x) * mu


def _gla_scan(q, k, v, log_f, s0):
    b, t, h, dv = v.shape
    q, k, v, log_f = (_to_chunks(a) for a in (q, k, v, log_f))
    cum = jnp.cumsum(log_f, axis=2)
    last = cum[:, :, -1:]
    q_dec = q * jnp.exp(cum)
    k_inv = k * jnp.exp(-cum)
    k_end = k * jnp.exp(last - cum)
    lower = jnp.tril(jnp.ones((CHUNK, CHUNK), bool))
    att = jnp.where(lower, jnp.einsum('bnihd,bnjhd->bnhij', q_dec, k_inv), 0.0)
    o_intra = jnp.einsum('bnhij,bnjhv->bnihv', att, v)
    d_state = jnp.einsum('bnjhd,bnjhv->nbhdv', k_end, v)
    chunk_decay = jnp.moveaxis(jnp.exp(last[:, :, 0]), 1, 0)

    def step(s, inp):
        dec, ds = inp
        return dec[..., None] * s + ds, s

    s_fin, s_start = lax.scan(step, s0, (chunk_decay, d_state))
    o_inter = jnp.einsum('bnihd,nbhdv->bnihv', q_dec, s_start)
    return (o_intra + o_inter).reshape(b, t, h, dv), s_fin


def _gdn_scan(q, k, v, log_a, beta, s0):
    b, t, h, dv = v.shape
    dk = q.shape[-1]
    hm = lambda a: jnp.moveaxis(_to_chunks(a), 3, 2)
    q, k, v, log_a, beta = (hm(a) for a in (q, k, v, log_a, beta))
    q = q * dk ** -0.5
    cum = jnp.cumsum(log_a, axis=-1)
    lower = jnp.tril(jnp.ones((CHUNK, CHUNK), bool))
    strict = jnp.tril(jnp.ones((CHUNK, CHUNK), bool), -1)
    decay = jnp.exp(jnp.where(lower, cum[..., :, None] - cum[..., None, :], -jnp.inf))
    kk = jnp.einsum('bnhid,bnhjd->bnhij', k, k)
    lmat = jnp.where(strict, beta[..., None] * kk * decay, 0.0) + jnp.eye(CHUNK, dtype=kk.dtype)
    rhs = jnp.concatenate([v * beta[..., None], k * (beta * jnp.exp(cum))[..., None]], axis=-1)
    sol = lax.linalg.triangular_solve(lmat, rhs, left_side=True, lower=True, unit_diagonal=True)
    u, w = sol[..., :dv], sol[..., dv:]
    a_qk = jnp.einsum('bnhid,bnhjd->bnhij', q, k) * decay
    k_end = k * jnp.exp(cum[..., -1:] - cum)[..., None]
    chunk_decay = jnp.exp(cum[..., -1])
    cm = lambda a: jnp.moveaxis(a, 1, 0)

    def step(s, inp):
        u_i, w_i, ke_i, dec_i = inp
        v_new = u_i - jnp.einsum('bhcd,bhdv->bhcv', w_i, s)
        s_next = dec_i[..., None, None] * s + jnp.einsum('bhcd,bhcv->bhdv', ke_i, v_new)
        return s_next, (s, v_new)

    s_fin, (s_start, v_new) = lax.scan(step, s0, (cm(u), cm(w), cm(k_end), cm(chunk_decay)))
    o = (jnp.einsum('bnhcd,nbhdv->bnhcv', q * jnp.exp(cum)[..., None], s_start)
         + jnp.einsum('bnhij,nbhjv->bnhiv', a_qk, v_new))
    return jnp.moveaxis(o, 2, 3).reshape(b, t, h, dv), s_fin


def _rwkv7_scan(r, w, k, v, kk, a, s0):
    tm = lambda t: jnp.moveaxis(t, 1, 0)

    def step(s, inp):
        r_t, w_t, k_t, v_t, kk_t, a_t = inp
        sa = jnp.einsum('bhvk,bhk->bhv', s, kk_t)
        s = (s * w_t[:, :, None, :] - sa[..., None] * (kk_t * a_t)[:, :, None, :]
             + v_t[..., None] * k_t[:, :, None, :])
        return s, jnp.einsum('bhvk,bhk->bhv', s, r_t)

    s_fin, y = lax.scan(step, s0, tuple(tm(t) for t in (r, w, k, v, kk, a)))
    return tm(y), s_fin


def _rwkv7_direction(p_rkv, p_wa, s0, mu_rkv, mu_wa, w0, w2, a0, a2, k_k, k_a, r_k):
    x_rkv = _token_shift(p_rkv, mu_rkv)
    x_wa = _token_shift(p_wa, mu_wa)
    r, k, v = jnp.split(x_rkv, 3, axis=-1)
    w_lo, a_lo = jnp.split(x_wa, [RW_DECAY_LR], axis=-1)
    decay = jnp.exp(-RW_DECAY_SCALE * jax.nn.sigmoid(w0 + jnp.tanh(w_lo) @ w2))
    a = jax.nn.sigmoid(a0 + a_lo @ a2)
    kk = _l2norm(_heads(k * k_k, RW_HEADS))
    k = k * (1.0 + (a - 1.0) * k_a)
    r, k, v, a, decay = (_heads(t, RW_HEADS) for t in (r, k, v, a, decay))
    y, s_fin = _rwkv7_scan(r, decay, k, v, kk, a, s0)
    bonus = jnp.sum(r * k * r_k, axis=-1, keepdims=True) * v
    return y, bonus, s_fin


def _mix(n, row_len, init, lp):
    b, t, _ = n.shape
    offs = tuple(int(o) for o in np.cumsum(PROJ_WIDTHS)[:-1])
    (g_q, g_k, g_v, g_r, g_lo, d_qkv, d_z, d_a, d_b, c_b, c_c, c_h, r_rkv, r_wa, r_g) = jnp.split(
        (n @ lp['w_in']).astype(jnp.float32), offs, axis=-1)

    q = _heads(g_q, GLA_HEADS) * GLA_DK ** -0.5
    k = _heads(g_k, GLA_HEADS)
    v = _heads(g_v, GLA_HEADS)
    log_f = [_heads(jax.nn.log_sigmoid(g_lo @ lp['gla_a_up'][i] + lp['gla_a_b'][i]) / GLA_TAU, GLA_HEADS)
             for i in range(2)]
    o_f, s_gla_f = _gla_scan(q, k, v, log_f[0], init[0])
    o_b, s_gla_b = _gla_scan(_flip(q), _flip(k), _flip(v), _flip(log_f[1]), init[1])
    y_gla = _rms_heads(o_f + _flip(o_b), lp['gla_norm_g']) * jax.nn.silu(_heads(g_r, GLA_HEADS))

    qkv = jax.nn.silu(_conv3_rows(d_qkv, lp['gdn_conv'], row_len))
    q, k, v = (_heads(a, GDN_HEADS) for a in jnp.split(qkv, 3, axis=-1))
    q, k = _l2norm(q), _l2norm(k)
    log_a = -jnp.exp(lp['gdn_a_log']) * jax.nn.softplus(d_a.reshape(b, t, 2, GDN_HEADS) + lp['gdn_dt_bias'])
    beta = jax.nn.sigmoid(d_b.reshape(b, t, 2, GDN_HEADS))
    o_f, s_gdn_f = _gdn_scan(q, k, v, log_a[:, :, 0], beta[:, :, 0], init[2])
    o_b, s_gdn_b = _gdn_scan(_flip(q), _flip(k), _flip(v), _flip(log_a[:, :, 1]), _flip(beta[:, :, 1]), init[3])
    y_gdn = _rms_heads(o_f + _flip(o_b), lp['gdn_norm_g']) * jax.nn.silu(_heads(d_z, GDN_HEADS))

    y_sc = c_b * _conv3_rows(c_c * c_h, lp['sc_conv'], row_len)

    def dir_args(i):
        return (lp['rw_mu_rkv'][i], lp['rw_mu_wa'][i], lp['rw_w0'][i], lp['rw_w2'][i],
                lp['rw_a0'][i], lp['rw_a2'][i], lp['rw_kk'], lp['rw_ka'], lp['rw_rk'])
    y_f, bo_f, s_rw_f = _rwkv7_direction(r_rkv, r_wa, init[4], *dir_args(0))
    y_b, bo_b, s_rw_b = _rwkv7_direction(_flip(r_rkv), _flip(r_wa), init[5], *dir_args(1))
    y_rw = (_group_norm(y_f + _flip(y_b), lp['rw_gn_w'], lp['rw_gn_b'])
            + (bo_f + _flip(bo_b)).reshape(b, t, GROUP))
    y_rw = y_rw * (jax.nn.sigmoid(r_g) @ lp['rw_g2'])

    y = jnp.concatenate([y_gla.reshape(b, t, GROUP), y_gdn.reshape(b, t, GROUP), y_sc, y_rw],
                        axis=-1).astype(n.dtype)
    return y @ lp['w_out'], (s_gla_f, s_gla_b, s_gdn_f, s_gdn_b, s_rw_f, s_rw_b)


def _modulation(cond, w, bias):
    m = jax.nn.silu(cond) @ w + bias
    return [a[:, None, :] for a in jnp.split(m, 6, axis=-1)]


def _modnorm(h, g, shift, scale):
    return _rmsnorm(h, g) * (1.0 + scale) + shift


def _swiglu(h, w_gu, w_down):
    gate, up = jnp.split(h @ w_gu, 2, axis=-1)
    return (jax.nn.silu(gate) * up) @ w_down


def _zero_states(b):
    z = lambda h, d1, d2: jnp.zeros((b, h, d1, d2), jnp.float32)
    return (z(GLA_HEADS, GLA_DK, GLA_DV), z(GLA_HEADS, GLA_DK, GLA_DV),
            z(GDN_HEADS, GDN_DK, GDN_DV), z(GDN_HEADS, GDN_DK, GDN_DV),
            z(RW_HEADS, RW_HEAD, RW_HEAD), z(RW_HEADS, RW_HEAD, RW_HEAD))


def setup_inputs(seed: int = 0) -> dict:
    key = jax.random.key(seed)
    keys = iter(jax.random.split(key, 48))

    def nrm(shape, std):
        return std * jax.random.normal(next(keys), shape, jnp.float32)

    def uni(shape, lo, hi):
        return jax.random.uniform(next(keys), shape, jnp.float32, lo, hi)

    L, D = DEPTH, D_MODEL
    dt = jnp.exp(uni((L, 2, GDN_HEADS), math.log(1e-3), math.log(1e-1)))
    return {
        'x': nrm((BATCH, SEQ, D), 1.0),
        'c': nrm((BATCH, D), 1.0),
        'ctx': nrm((BATCH, CTX_LEN, D), 1.0),
        'c_ctx': nrm((D,), 1.0),
        'ada_w': nrm((L, D, 6 * D), 0.5 * D ** -0.5),
        'ada_b': nrm((L, 6 * D), 0.02),
        'norm1_g': 1.0 + nrm((L, D), 0.02),
        'norm2_g': 1.0 + nrm((L, D), 0.02),
        'w_in': nrm((L, D, PROJ), D ** -0.5),
        'w_out': nrm((L, MIX, D), MIX ** -0.5),
        'gla_a_up': nrm((L, 2, GLA_LR, GLA_HEADS * GLA_DK), GLA_LR ** -0.5),
        'gla_a_b': nrm((L, 2, GLA_HEADS * GLA_DK), 0.5),
        'gla_norm_g': 1.0 + nrm((L, GLA_DV), 0.02),
        'gdn_conv': nrm((L, CONV_W, 3 * GROUP), 0.5),
        'gdn_a_log': jnp.log(uni((L, 2, GDN_HEADS), 1.0, 16.0)),
        'gdn_dt_bias': dt + jnp.log(-jnp.expm1(-dt)),
        'gdn_norm_g': 1.0 + nrm((L, GDN_DV), 0.02),
        'sc_conv': nrm((L, CONV_W, GROUP), 0.5),
        'rw_mu_rkv': uni((L, 2, 3 * GROUP), 0.0, 1.0),
        'rw_mu_wa': uni((L, 2, RW_DECAY_LR + RW_A_LR), 0.0, 1.0),
        'rw_w0': uni((L, 2, GROUP), -2.0, 2.0),
        'rw_w2': nrm((L, 2, RW_DECAY_LR, GROUP), RW_DECAY_LR ** -0.5),
        'rw_a0': nrm((L, 2, GROUP), 0.5),
        'rw_a2': nrm((L, 2, RW_A_LR, GROUP), RW_A_LR ** -0.5),
        'rw_g2': nrm((L, RW_GATE_LR, GROUP), RW_GATE_LR ** -0.5),
        'rw_kk': 0.85 + nrm((L, GROUP), 0.02),
        'rw_ka': 1.0 + nrm((L, GROUP), 0.02),
        'rw_rk': nrm((L, RW_HEADS, RW_HEAD), 0.1),
        'rw_gn_w': 1.0 + nrm((L, GROUP), 0.02),
        'rw_gn_b': nrm((L, GROUP), 0.02),
        'ffn_w_gu': nrm((L, D, 2 * FFN), D ** -0.5),
        'ffn_w_down': nrm((L, FFN, D), FFN ** -0.5),
        'final_g': 1.0 + nrm((D,), 0.02),
    }


def reference(x, c, ctx, c_ctx, ada_w, ada_b, norm1_g, norm2_g, w_in, w_out,
              gla_a_up, gla_a_b, gla_norm_g, gdn_conv, gdn_a_log, gdn_dt_bias, gdn_norm_g,
              sc_conv, rw_mu_rkv, rw_mu_wa, rw_w0, rw_w2, rw_a0, rw_a2, rw_g2, rw_kk, rw_ka,
              rw_rk, rw_gn_w, rw_gn_b, ffn_w_gu, ffn_w_down, final_g):
    h_x, h_c = x, ctx
    zero = _zero_states(x.shape[0])
    for l in range(DEPTH):
        lp = dict(w_in=w_in[l], w_out=w_out[l], gla_a_up=gla_a_up[l], gla_a_b=gla_a_b[l],
                  gla_norm_g=gla_norm_g[l], gdn_conv=gdn_conv[l], gdn_a_log=gdn_a_log[l],
                  gdn_dt_bias=gdn_dt_bias[l], gdn_norm_g=gdn_norm_g[l], sc_conv=sc_conv[l],
                  rw_mu_rkv=rw_mu_rkv[l], rw_mu_wa=rw_mu_wa[l], rw_w0=rw_w0[l], rw_w2=rw_w2[l],
                  rw_a0=rw_a0[l], rw_a2=rw_a2[l], rw_g2=rw_g2[l], rw_kk=rw_kk[l], rw_ka=rw_ka[l],
                  rw_rk=rw_rk[l], rw_gn_w=rw_gn_w[l], rw_gn_b=rw_gn_b[l])
        m_x = _modulation(c, ada_w[l], ada_b[l])
        m_c = _modulation(c_ctx[None, :], ada_w[l], ada_b[l])
        o_c, ctx_states = _mix(_modnorm(h_c, norm1_g[l], m_c[0], m_c[1]), h_c.shape[1], zero, lp)
        o_x, _ = _mix(_modnorm(h_x, norm1_g[l], m_x[0], m_x[1]), GRID_W, ctx_states, lp)
        h_x = h_x + m_x[2] * o_x
        h_x = h_x + m_x[5] * _swiglu(_modnorm(h_x, norm2_g[l], m_x[3], m_x[4]), ffn_w_gu[l], ffn_w_down[l])
        if l < DEPTH - 1:
            h_c = h_c + m_c[2] * o_c
            h_c = h_c + m_c[5] * _swiglu(_modnorm(h_c, norm2_g[l], m_c[3], m_c[4]), ffn_w_gu[l], ffn_w_down[l])
    return _rmsnorm(h_x, final_g)
```

```python
import math
import numpy as np
import concourse.bass as bass
import concourse.mybir as mybir
from concourse.bass_utils import run_bass_kernel_spmd

F32 = mybir.dt.float32
BF16 = mybir.dt.bfloat16
AF = mybir.ActivationFunctionType
ALU = mybir.AluOpType
AX = mybir.AxisListType

import os as _os
SEM_EPOCH = int(_os.environ.get('SEM_EPOCH', 8000))
P1_GRP = int(_os.environ.get('P1_GRP', 2304))
C = 128
D = 2048
KC = 16
FFN = 5632
FC = FFN // 128
NF = 5504
NM = 1808
EPS = 1e-6
NEG = -1.0e5
RW_C = math.exp(-0.5)


class Res:
    __slots__ = ("name", "w", "r", "multi", "excl")

    def __init__(self, name, multi=False):
        self.name = name
        self.w = {}
        self.r = {}
        self.multi = multi
        self.excl = False


class T:
    def __init__(self, th, name):
        self.th = th
        self.res = Res(name)

    def __getitem__(self, idx):
        return self.th[idx]


def _res(x):
    return x.res if hasattr(x, "res") else x


class Ring:
    def __init__(self, tiles):
        self.tiles = tiles
        self.i = 0

    def next(self):
        t = self.tiles[self.i]
        self.i = (self.i + 1) % len(self.tiles)
        return t


class Prog:
    def __init__(self, nc, n_dma_sems=32):
        self.nc = nc
        self.engs = {"pe": nc.tensor, "dve": nc.vector, "act": nc.scalar,
                     "pool": nc.gpsimd, "sp": nc.sync}
        self.gstack = []
        self.scopes = []
        self.sems = {}
        self.cnt = {e: 0 for e in self.engs}
        self.epoch = {e: 0 for e in self.engs}
        self.waited = {e: {} for e in self.engs}
        for e in self.engs:
            self._new_sem(e, 0)
        self.dsems = []
        for i in range(n_dma_sems):
            cm = nc.semaphore(f"dma{i}")
            self.dsems.append(cm.__enter__())
            self.gstack.append(cm)
        self.duse = [0] * n_dma_sems
        self.dnext = {"sp": 0, "pool": 0}
        self.n_inst = 0
        self.uid = 0

    def _new_sem(self, e, ep):
        cm = self.nc.semaphore(f"s_{e}_{ep}")
        self.sems[(e, ep)] = cm.__enter__()
        self.gstack.append(cm)

    def _push(self, cm):
        (self.scopes[-1] if self.scopes else self.gstack).append(cm)

    def sbuf(self, name, shape, dt):
        self.uid += 1
        nm = f"{name}_{self.uid}"
        cm = self.nc.sbuf_tensor(nm, list(shape), dt)
        th = cm.__enter__()
        self._push(cm)
        return T(th, nm)

    def ring(self, name, shape, dt, n):
        return Ring([self.sbuf(f"{name}{i}", shape, dt) for i in range(n)])

    def psum(self, name, shape, dt=F32):
        cm = self.nc.psum_tensor(name, list(shape), dt)
        th = cm.__enter__()
        self._push(cm)
        t = T(th, name)
        t.res.excl = True
        return t

    def push_scope(self):
        self.scopes.append([])

    def barrier(self):
        for e in self.engs:
            for (src, ep), sem in list(self.sems.items()):
                if ep != self.epoch[src] or src == e:
                    continue
                if self.cnt[src] > 0:
                    self._need(e, ("e", (src, ep), self.cnt[src]))
            for i in range(len(self.dsems)):
                if self.duse[i] > 0:
                    self._need(e, ("d", i, 16 * self.duse[i]))

    def pop_scope(self):
        self.barrier()
        sc = self.scopes.pop()
        for cm in reversed(sc):
            cm.__exit__(None, None, None)

    def close(self):
        for cm in reversed(self.gstack):
            cm.__exit__(None, None, None)

    def _need(self, eng, dep):
        kind, key, count = dep
        w = self.waited[eng]
        k = (kind, key)
        if w.get(k, 0) >= count:
            return
        w[k] = count
        sem = self.sems[key] if kind == "e" else self.dsems[key]
        self.engs[eng].wait_ge(sem, count)

    def _deps(self, eng, reads, writes, same_engine_ok=False):
        deps = []
        for r in reads:
            res = _res(r)
            deps.extend(res.w.values())
            if res.excl:
                deps.extend(d for k, d in res.r.items() if k != ("e", eng))
        for wv in writes:
            res = _res(wv)
            if not res.multi:
                deps.extend(res.w.values())
            deps.extend(res.r.values())
        for d in deps:
            if same_engine_ok and d[0] == "e" and d[1][0] == eng:
                continue
            self._need(eng, d)

    def _mark(self, tag, reads, writes):
        key = ("d", tag[1]) if tag[0] == "d" else ("e", tag[1][0])
        for r in reads:
            _res(r).r[key] = tag
        for wv in writes:
            res = _res(wv)
            if res.multi:
                res.w[key] = tag
            else:
                res.w = {key: tag}
                res.r = {}

    def op(self, eng, fn, reads=(), writes=(), same_engine_ok=False):
        self._deps(eng, reads, writes, same_engine_ok)
        if self.cnt[eng] >= SEM_EPOCH:
            self.epoch[eng] += 1
            self.cnt[eng] = 0
            self._new_sem(eng, self.epoch[eng])
        ins = fn()
        self.cnt[eng] += 1
        key = (eng, self.epoch[eng])
        ins.then_inc(self.sems[key], 1)
        tag = ("e", key, self.cnt[eng])
        self._mark(tag, reads, writes)
        self.n_inst += 1
        return tag

    def dma(self, q, out, in_, reads=(), writes=(), **kw):
        half = len(self.dsems) // 2
        base = 0 if q == "sp" else half
        i = base + self.dnext[q]
        self.dnext[q] = (self.dnext[q] + 1) % half
        if self.duse[i] > 0:
            self._need(q, ("d", i, 16 * self.duse[i]))
        self._deps(q, reads, writes)
        ins = self.engs[q].dma_start(out=out, in_=in_, **kw)
        self.duse[i] += 1
        ins.then_inc(self.dsems[i], 16)
        tag = ("d", i, 16 * self.duse[i])
        self._mark(tag, reads, writes)
        self.n_inst += 1
        return tag


class K:
    def __init__(self, seq, ctx, depth, debug=False):
        self.SEQ, self.CTX, self.L = seq, ctx, depth
        self.S = seq + ctx
        self.debug = debug
        nc = bass.Bass("TRN2", target_bir_lowering=False)
        self.nc = nc
        self.p = Prog(nc)
        self.dram = {}
        self.dres = {}
        self.blocks = [(0, ctx, 1)]
        tt = min(512, seq)
        for i in range(seq // tt):
            self.blocks.append((ctx + i * tt, tt, 0))
        self.chunks = {1: [i * C for i in range(ctx // C)], 0: [ctx + i * C for i in range(seq // C)]}

    def din(self, name, shape, dt=F32):
        t = self.nc.dram_tensor(name, list(shape), dt, kind="ExternalInput")
        self.dram[name] = t
        self.dres[name] = Res(name, multi=True)
        return t

    def dscr(self, name, shape, dt=F32, out=False):
        kind = "ExternalOutput" if (out or self.debug) else "Internal"
        t = self.nc.dram_tensor(name, list(shape), dt, kind=kind)
        self.dram[name] = t
        self.dres[name] = Res(name, multi=True)
        return t

    def load(self, dst_ap, dst_tile, name, src_ap, q="sp"):
        return self.p.dma(q, dst_ap, src_ap, reads=[self.dres[name]], writes=[dst_tile])

    def store(self, name, dst_ap, src_ap, src_tile, q="pool"):
        return self.p.dma(q, dst_ap, src_ap, reads=[src_tile], writes=[self.dres[name]])

    def dbg(self, name, ap, tile, shape):
        if not self.debug or ("dbg_" + name) in self.dram:
            return
        self.dscr("dbg_" + name, shape, out=True)
        self.store("dbg_" + name, self.dram["dbg_" + name].ap(), ap, tile)

    def mm(self, out, lhsT, rhs, start, stop, reads, writes):
        nc = self.nc
        return self.p.op("pe", lambda: nc.tensor.matmul(out, lhsT=lhsT, rhs=rhs, start=start, stop=stop),
                         reads=reads, writes=writes, same_engine_ok=True)

    def tr(self, out, in_, reads, writes):
        nc = self.nc
        idn = self.ident
        return self.p.op("pe", lambda: nc.tensor.transpose(out, in_, idn[:]),
                         reads=list(reads) + [idn], writes=writes, same_engine_ok=True)

    def act(self, out, in_, func, reads, writes, bias=None, scale=None, accum_out=None):
        nc = self.nc
        kw = {}
        if bias is not None:
            kw["bias"] = bias
        if scale is not None:
            kw["scale"] = scale
        if accum_out is not None:
            kw["accum_out"] = accum_out
        return self.p.op("act", lambda: nc.scalar.activation(out=out, in_=in_, func=func, **kw),
                         reads=reads, writes=writes)

    def tt(self, out, in0, in1, op, reads, writes, eng="dve"):
        e = self.p.engs[eng]
        return self.p.op(eng, lambda: e.tensor_tensor(out=out, in0=in0, in1=in1, op=op), reads=reads, writes=writes)

    def ts(self, out, in0, s1, s2, op0, op1, reads, writes, eng="dve"):
        e = self.p.engs[eng]
        if s2 is None:
            return self.p.op(eng, lambda: e.tensor_scalar(out=out, in0=in0, scalar1=s1, scalar2=None, op0=op0),
                             reads=reads, writes=writes)
        return self.p.op(eng, lambda: e.tensor_scalar(out=out, in0=in0, scalar1=s1, scalar2=s2, op0=op0, op1=op1),
                         reads=reads, writes=writes)

    def stt(self, out, in0, scalar, in1, op0, op1, reads, writes):
        nc = self.nc
        return self.p.op("dve", lambda: nc.vector.scalar_tensor_tensor(out=out, in0=in0, scalar=scalar, in1=in1,
                                                                         op0=op0, op1=op1), reads=reads, writes=writes)

    def copy(self, out, in_, reads, writes, eng="dve"):
        if eng == "act":
            return self.act(out, in_, AF.Copy, reads, writes)
        e = self.p.engs[eng]
        return self.p.op(eng, lambda: e.tensor_copy(out=out, in_=in_), reads=reads, writes=writes)

    def recip(self, out, in_, reads, writes):
        nc = self.nc
        return self.p.op("dve", lambda: nc.vector.reciprocal(out=out, in_=in_), reads=reads, writes=writes)

    def memset(self, tile, ap, val, eng="pool"):
        e = self.p.engs[eng]
        return self.p.op(eng, lambda: e.memset(ap, val), writes=[tile])

    def tri(self, tile, ap, d, strict, fill):
        nc = self.nc
        op = ALU.is_gt if strict else ALU.is_ge
        if d == 0:
            pat, cm = [[1, 128]], -1
        else:
            pat, cm = [[-1, 128]], 1
        return self.p.op("pool", lambda: nc.gpsimd.affine_select(out=ap, in_=ap, pattern=pat, compare_op=op,
                                                                 fill=fill, base=0, channel_multiplier=cm),
                         reads=[tile], writes=[tile])

    def consts(self):
        p, nc = self.p, self.nc
        self.ident = p.sbuf("ident", [128, 128], F32)
        self.memset(self.ident, self.ident[:], 1.0)
        p.op("pool", lambda: nc.gpsimd.affine_select(out=self.ident[:], in_=self.ident[:], pattern=[[-1, 128]],
                                                     compare_op=ALU.is_equal, fill=0.0, base=0, channel_multiplier=1),
             reads=[self.ident], writes=[self.ident])
        self.ones = p.sbuf("ones", [128, 128], F32)
        self.memset(self.ones, self.ones[:], 1.0)
        self.nones = p.sbuf("nones", [128, 128], F32)
        self.memset(self.nones, self.nones[:], -1.0)
        self.bones = p.sbuf("bones", [128, 128], F32)
        self.memset(self.bones, self.bones[:], 0.0)
        self.memset(self.bones, self.bones[0:64, 0:64], 1.0)
        self.memset(self.bones, self.bones[64:128, 64:128], 1.0)
        self.triI, self.triS, self.triIm1, self.negS, self.negI = [], [], [], [], []
        for d in (0, 1):
            t = p.sbuf(f"triI{d}", [128, 129], F32)
            self.memset(t, t[:], 1.0)
            self.tri(t, t[:, 0:128], d, False, 0.0)
            self.triI.append(t)
            t = p.sbuf(f"triS{d}", [128, 128], F32)
            self.memset(t, t[:], 1.0)
            self.tri(t, t[:], d, True, 0.0)
            self.triS.append(t)
            t = p.sbuf(f"triIm1{d}", [128, 128], F32)
            self.memset(t, t[:], 0.0)
            self.tri(t, t[:], d, False, -1.0)
            self.triIm1.append(t)
            t = p.sbuf(f"negI{d}", [128, 128], F32)
            self.memset(t, t[:], 0.0)
            self.tri(t, t[:], d, False, NEG)
            self.negI.append(t)
        for d in (0, 1):
            t = p.sbuf(f"negS{d}", [128, 128], F32)
            self.memset(t, t[:], 0.0)
            self.tri(t, t[:], 1 - d, True, NEG)
            self.negS.append(t)
        self.psb = [p.psum(f"psb{i}", [128, 512]) for i in range(8)]
        self.psi = 0

    def ps(self):
        t = self.psb[self.psi]
        self.psi = (self.psi + 1) % 7
        return t

    def declare(self):
        L, S = self.L, self.S
        self.din("xT", [D, S])
        self.din("cT", [128, KC, 2])
        self.din("ada_w", [L, D, 6 * D])
        self.din("ada_bT", [L, 128, 96])
        self.din("g1T", [L, 128, KC])
        self.din("g2T", [L, 128, KC])
        self.din("gfT", [128, KC])
        self.din("w_inF", [L, D, NF])
        self.din("w_inM", [L, D, NM])
        self.din("w_out", [L, D, D])
        self.din("w_gu", [L, D, 2 * FFN])
        self.din("w_down", [L, FFN, D])
        self.din("gla_aw", [L, 2, 17, 256])
        self.din("gla_g", [L, 128, 128])
        self.din("gdn_convT", [L, 128, 12, 3])
        self.din("gdn_ab", [L, 128, 16])
        self.din("gdn_g", [L, 128, 128])
        self.din("sc_convT", [L, 128, 4, 3])
        self.din("rw_mu", [L, 2, 128, 14])
        self.din("rw_w0a0", [L, 2, 128, 8])
        self.din("rw_w2a2", [L, 2, 32, 1024])
        self.din("rw_g2", [L, 96, 512])
        self.din("rw_vec", [L, 128, 20])
        self.dscr("out", [D, self.SEQ], out=True)
        self.dscr("hA", [D, S])
        self.dscr("pF", [NF, S])
        self.dscr("pM", [S, NM])
        self.dscr("oF", [3, S, 512])
        self.dscr("bnF", [512, S])
        self.dscr("yT", [D, S], BF16)

    def phase_mod(self):
        p, nc, L = self.p, self.nc, self.L
        d = self.dram
        self.modT = [p.sbuf(f"modT{l}", [128, 2, 96], F32) for l in range(L)]
        self.G1 = [p.sbuf(f"G1_{l}", [128, 2, KC], F32) for l in range(L)]
        self.G2 = [p.sbuf(f"G2_{l}", [128, 2, KC], F32) for l in range(L)]
        self.gfT = p.sbuf("gfT", [128, KC], F32)
        self.load(self.gfT[:], self.gfT, "gfT", d["gfT"].ap())
        p.push_scope()
        cT = p.sbuf("cT", [128, KC, 2], F32)
        sc = p.sbuf("sc", [128, KC, 2], F32)
        self.load(cT[:], cT, "cT", d["cT"].ap())
        self.act(sc[:], cT[:], AF.Silu, [cT], [sc])
        wr = p.ring("adaw", [128, KC, 512], F32, 2)
        bT = p.sbuf("bT", [128, 96], F32)
        gt = p.sbuf("gt", [128, KC], F32)
        for l in range(L):
            self.load(bT[:], bT, "ada_bT", d["ada_bT"].ap()[l])
            pm = self.ps()
            for jt in range(24):
                wt = wr.next()
                self.load(wt[:], wt, "ada_w",
                          d["ada_w"].ap()[l, :, jt * 512:(jt + 1) * 512].rearrange("(kc p) n -> p kc n", p=128))
                for sub in range(4):
                    j = jt * 4 + sub
                    for kc in range(KC):
                        self.mm(pm[:, 2 * j:2 * j + 2], wt[:, kc, sub * 128:(sub + 1) * 128], sc[:, kc, :],
                                kc == 0, kc == KC - 1, [wt, sc], [pm])
            pv = pm[:, 0:192].rearrange("p (j r) -> p j r", r=2)
            for r in range(2):
                self.tt(self.modT[l][:, r, :], pv[:, :, r], bT[:], ALU.add, [pm, bT], [self.modT[l]])
            for (G, gname, sc0) in ((self.G1[l], "g1T", 16), (self.G2[l], "g2T", 64)):
                self.load(gt[:], gt, gname, d[gname].ap()[l])
                for r in range(2):
                    self.stt(G[:, r, :], self.modT[l][:, r, sc0:sc0 + KC], 1.0, gt[:], ALU.add, ALU.mult,
                             [self.modT[l], gt], [G])
        p.pop_scope()

    def rms_rstd(self, ht, n, sqr, rstd):
        pt = self.ps()
        for kc in range(KC):
            sq = sqr.next()
            self.act(sq[:, :n], ht[:, kc, :n], AF.Square, [ht], [sq])
            self.mm(pt[:, :n], self.ones[:], sq[:, :n], kc == 0, kc == KC - 1, [self.ones, sq], [pt])
        self.act(rstd[:, :n], pt[:, :n], AF.Sqrt, [pt], [rstd], bias=self.eps_t[:, 0:1], scale=1.0 / D)
        self.recip(rstd[:, :n], rstd[:, :n], [rstd], [rstd])

    def modnorm(self, ht, n, G, shiftT, sh0, r, dst, dcol, sqr, rstd, tmpr):
        self.rms_rstd(ht, n, sqr, rstd)
        for kc in range(KC):
            tmp = tmpr.next()
            self.stt(tmp[:, :n], ht[:, kc, :n], G[:, r, kc:kc + 1], rstd[:, :n], ALU.mult, ALU.mult,
                     [ht, G, rstd], [tmp])
            self.act(dst[:, kc, dcol:dcol + n], tmp[:, :n], AF.Identity, [tmp, shiftT], [dst],
                     bias=shiftT[:, r, sh0 + kc:sh0 + kc + 1])

    def conv3(self, dst_t, dst, src, src_deps, taps_t, taps, n, rl):
        self.ts(dst, src, taps[:, 1:2], None, ALU.mult, None, list(src_deps) + [taps_t], [dst_t])
        dv = dst.rearrange("p (r c) -> p r c", c=rl)
        sv = src.rearrange("p (r c) -> p r c", c=rl)
        self.stt(dv[:, :, 1:rl], sv[:, :, 0:rl - 1], taps[:, 0:1], dv[:, :, 1:rl], ALU.mult, ALU.add,
                 list(src_deps) + [taps_t, dst_t], [dst_t])
        self.stt(dv[:, :, 0:rl - 1], sv[:, :, 1:rl], taps[:, 2:3], dv[:, :, 0:rl - 1], ALU.mult, ALU.add,
                 list(src_deps) + [taps_t, dst_t], [dst_t])

    def phase_p1(self, l, hname):
        p, nc, d = self.p, self.nc, self.dram
        last = (l == self.L - 1)
        groups, cur, tot = [], [], 0
        for b in self.blocks:
            if tot + b[1] > P1_GRP:
                groups.append(cur)
                cur, tot = [], 0
            cur.append(b)
            tot += b[1]
        groups.append(cur)
        ftiles = [[0, 1, 2, 3], [4, 5, 6, 7], [8, 9, 10, 11], [12, 13, 14, 15], [16, 17, 18], [19, 20, 21],
                  [22, 23, 24], [25, 26, 27], [28, 29, 30, 31], [32, 33, 34, 35], [36, 37, 38, 39], [40, 41, 42]]
        mtiles = [(0, 512), (512, 512), (1024, 512), (1536, 272)]
        for grp in groups:
            p.push_scope()
            ntok = sum(b[1] for b in grp)
            nT = p.sbuf("nT", [128, KC, ntok], BF16)
            gcv = p.sbuf("gcv", [128, 12, 3], F32)
            scv = p.sbuf("scv", [128, 4, 3], F32)
            self.load(gcv[:], gcv, "gdn_convT", d["gdn_convT"].ap()[l])
            self.load(scv[:], scv, "sc_convT", d["sc_convT"].ap()[l])
            p.push_scope()
            htr = p.ring("ht", [128, KC, 512], F32, 2)
            sqr = p.ring("sq", [128, 512], F32, 3)
            tmpr = p.ring("ntmp", [128, 512], F32, 3)
            rstd = p.sbuf("rstd", [128, 512], F32)
            col = 0
            for (t0, n, seg) in grp:
                ht = htr.next()
                self.load(ht[:, :, :n], ht, hname,
                          d[hname].ap()[:, t0:t0 + n].rearrange("(kc p) t -> p kc t", p=128))
                self.modnorm(ht, n, self.G1[l], self.modT[l], 0, seg, nT, col, sqr, rstd, tmpr)
                col += n
            p.pop_scope()
            p.push_scope()
            wr = p.ring("wF", [128, KC, 512], BF16, 2)
            stg = p.ring("stg", [128, 512], F32, 4)
            stb = p.ring("stb", [128, 512], BF16, 2)
            tmpa = p.ring("tmpa", [128, 512], F32, 2)
            tmpb = p.ring("tmpb", [128, 512], F32, 2)
            ei = 0
            for ft in ftiles:
                c0, ncol = ft[0] * 128, len(ft) * 128
                wt = wr.next()
                self.load(wt[:, :, :ncol], wt, "w_inF",
                          d["w_inF"].ap()[l, :, c0:c0 + ncol].rearrange("(kc p) n -> p kc n", p=128), q="pool")
                col = 0
                for (t0, n, seg) in grp:
                    rl = 64 if seg == 0 else self.CTX
                    pss = []
                    for gi, g in enumerate(ft):
                        pt = self.ps()
                        for kc in range(KC):
                            self.mm(pt[:, :n], wt[:, kc, gi * 128:(gi + 1) * 128], nT[:, kc, col:col + n],
                                    kc == 0, kc == KC - 1, [wt, nT], [pt])
                        pss.append(pt)
                    if 16 <= ft[0] <= 27:
                        gq = (ft[0] - 16) // 3
                        cb, cc, ch = pss
                        ta, tb = tmpa.next(), tmpb.next()
                        self.copy(ta[:, :n], ch[:, :n], [ch], [ta], eng="act")
                        self.tt(ta[:, :n], cc[:, :n], ta[:, :n], ALU.mult, [cc, ta], [ta])
                        self.conv3(tb, tb[:, :n], ta[:, :n], [ta], scv, scv[:, gq, :], n, rl)
                        sb = stb.next()
                        self.tt(sb[:, :n], cb[:, :n], tb[:, :n], ALU.mult, [cb, tb], [sb])
                        self.store("yT", d["yT"].ap()[1024 + gq * 128:1024 + (gq + 1) * 128, t0:t0 + n], sb[:, :n], sb,
                                   q="sp")
                    else:
                        for gi, g in enumerate(ft):
                            pt = pss[gi]
                            st = stg.next()
                            if 4 <= g <= 15:
                                ta = tmpa.next()
                                self.conv3(ta, ta[:, :n], pt[:, :n], [pt], gcv, gcv[:, g - 4, :], n, rl)
                                if g <= 11:
                                    tb = tmpb.next()
                                    self.act(tb[:, :n], ta[:, :n], AF.Silu, [ta], [tb])
                                    self.act(ta[:, :n], tb[:, :n], AF.Square, [tb], [ta])
                                    p2 = self.ps()
                                    self.mm(p2[:, :n], self.ones[:], ta[:, :n], True, True, [self.ones, ta], [p2])
                                    self.act(ta[:, :n], p2[:, :n], AF.Sqrt, [p2], [ta], bias=self.eps_t[:, 0:1])
                                    self.recip(ta[:, :n], ta[:, :n], [ta], [ta])
                                    self.tt(st[:, :n], tb[:, :n], ta[:, :n], ALU.mult, [tb, ta], [st])
                                else:
                                    self.act(st[:, :n], ta[:, :n], AF.Silu, [ta], [st])
                            else:
                                self.copy(st[:, :n], pt[:, :n], [pt], [st], eng=("act" if ei % 2 else "dve"))
                                ei += 1
                            self.store("pF", d["pF"].ap()[g * 128:(g + 1) * 128, t0:t0 + n], st[:, :n], st, q="sp")
                    col += n
            for (c0, ncol) in mtiles:
                wt = wr.next()
                self.load(wt[:, :, :ncol], wt, "w_inM",
                          d["w_inM"].ap()[l, :, c0:c0 + ncol].rearrange("(kc p) n -> p kc n", p=128), q="pool")
                col = 0
                for (t0, n, seg) in grp:
                    for sbk in range(n // 128):
                        pt = self.ps()
                        for kc in range(KC):
                            self.mm(pt[:, :ncol], nT[:, kc, col + sbk * 128:col + (sbk + 1) * 128], wt[:, kc, :ncol],
                                    kc == 0, kc == KC - 1, [wt, nT], [pt])
                        st = stg.next()
                        self.copy(st[:, :ncol], pt[:, :ncol], [pt], [st], eng=("act" if ei % 2 else "dve"))
                        ei += 1
                        self.store("pM", d["pM"].ap()[t0 + sbk * 128:t0 + (sbk + 1) * 128, c0:c0 + ncol],
                                   st[:, :ncol], st, q="sp")
                    col += n
            p.pop_scope()
            p.pop_scope()

    def chunk_order(self, dr):
        out = []
        for seg in (1, 0):
            cs = self.chunks[seg]
            cs = cs if dr == 0 else cs[::-1]
            for i, c0 in enumerate(cs):
                out.append((seg, c0, i == 0, i == len(cs) - 1))
        return out

    def finalize_rms_gate(self, o, gate_src, gbc, yrow, c0, rings):
        p, d = self.p, self.dram
        sqj, ssq, sg, yb = rings["sqj"].next(), rings["ssq"].next(), rings["sg"].next(), rings["yb"].next()
        for h in range(4):
            self.act(sqj[:, :], o[:, h * 128:(h + 1) * 128], AF.Square, [o], [sqj, ssq], accum_out=ssq[:, h:h + 1])
        self.act(ssq[:, 0:4], ssq[:, 0:4], AF.Sqrt, [ssq], [ssq], bias=self.eps_t[:, 0:1], scale=1.0 / 128)
        self.recip(ssq[:, 0:4], ssq[:, 0:4], [ssq], [ssq])
        self.act(sg[:], gate_src[:, 0:512], AF.Silu, [gate_src], [sg])
        for h in range(4):
            self.tt(sg[:, h * 128:(h + 1) * 128], sg[:, h * 128:(h + 1) * 128], gbc[:], ALU.mult, [sg, gbc], [sg])
        for h in range(4):
            self.stt(o[:, h * 128:(h + 1) * 128], o[:, h * 128:(h + 1) * 128], ssq[:, h:h + 1],
                     sg[:, h * 128:(h + 1) * 128], ALU.mult, ALU.mult, [o, ssq, sg], [o])
        pt = self.ps()
        for h in range(4):
            self.tr(pt[:, h * 128:(h + 1) * 128], o[:, h * 128:(h + 1) * 128], [o], [pt])
        self.copy(yb[:], pt[:], [pt], [yb], eng="act")
        self.store("yT", d["yT"].ap()[yrow:yrow + 512, c0:c0 + C].rearrange("(h p) t -> p h t", p=128),
                   yb[:].rearrange("p (h t) -> p h t", t=C), yb)

    def fin_rings(self):
        p = self.p
        return {"sqj": p.ring("sqj", [128, 128], F32, 2), "ssq": p.ring("ssq", [128, 4], F32, 2),
                "sg": p.ring("sg", [128, 512], F32, 2), "yb": p.ring("yb", [128, 512], BF16, 2)}

    def phase_gla(self, l):
        p, nc, d = self.p, self.nc, self.dram
        p.push_scope()
        aw = p.sbuf("aw", [17, 2, 256], F32)
        self.load(aw[:], aw, "gla_aw", d["gla_aw"].ap()[l].rearrange("d k n -> k d n"))
        gbc = p.sbuf("gbc", [128, 128], F32)
        self.load(gbc[:], gbc, "gla_g", d["gla_g"].ap()[l])
        glo = p.ring("glo", [32, C], F32, 2)
        for t in glo.tiles:
            self.memset(t, t[:], 1.0)
        qTr = p.ring("qT", [128, 2, C], F32, 2)
        kTr = p.ring("kT", [128, 2, C], F32, 2)
        tmr = p.ring("tm", [128, 1280], F32, 2)
        lr = p.ring("l", [128, 256], F32, 2)
        fmr = p.ring("fm", [128, 2, 3, C], F32, 2)
        decr = p.ring("dec", [128, 2], F32, 2)
        qdr = p.ring("qd", [128, 2, C], F32, 2)
        kir = p.ring("ki", [128, 2, C], F32, 2)
        ker = p.ring("ke", [128, 256], F32, 2)
        attr = p.ring("att", [128, C], F32, 3)
        osr = p.ring("os", [128, 512], F32, 2)
        ofr = p.ring("of", [128, 512], F32, 2)
        fr = self.fin_rings()
        S = p.sbuf("S", [128, 2, 256], F32)
        for dr in (0, 1):
            self.memset(S, S[:], 0.0)
            for (seg, c0, first, lastc) in self.chunk_order(dr):
                g1 = glo.next()
                self.load(g1[0:16, :], g1, "pF", d["pF"].ap()[42 * 128:42 * 128 + 16, c0:c0 + C])
                qT, kT, tm = qTr.next(), kTr.next(), tmr.next()
                self.load(qT[:], qT, "pF", d["pF"].ap()[0:256, c0:c0 + C].rearrange("(g p) t -> p g t", p=128))
                self.load(kT[:], kT, "pF", d["pF"].ap()[256:512, c0:c0 + C].rearrange("(g p) t -> p g t", p=128))
                self.load(tm[:], tm, "pM", d["pM"].ap()[c0:c0 + C, 0:1280])
                pz = self.ps()
                self.mm(pz[:, 0:256], g1[0:17, :], aw[:, dr, :], True, True, [g1, aw], [pz])
                lt = lr.next()
                self.act(lt[:], pz[:, 0:256], AF.Exp, [pz], [lt], scale=-1.0)
                self.act(lt[:], lt[:], AF.Ln, [lt], [lt], bias=self.one_t[:, 0:1])
                fm, dec = fmr.next(), decr.next()
                for hh in range(2):
                    pc = self.ps()
                    self.mm(pc[:, 0:129], lt[:, hh * 128:(hh + 1) * 128], self.triI[dr][:], True, True,
                            [lt, self.triI[dr]], [pc])
                    self.act(fm[:, hh, 0, :], pc[:, 0:128], AF.Exp, [pc], [fm], scale=-1.0 / 16)
                    self.act(fm[:, hh, 1, :], pc[:, 0:128], AF.Exp, [pc], [fm], scale=1.0 / 16)
                    self.act(dec[:, hh:hh + 1], pc[:, 128:129], AF.Exp, [pc], [dec], scale=-1.0 / 16)
                qd, ki = qdr.next(), kir.next()
                for hh in range(2):
                    self.stt(qd[:, hh, :], qT[:, hh, :], 0.125, fm[:, hh, 0, :], ALU.mult, ALU.mult, [qT, fm], [qd])
                    self.tt(ki[:, hh, :], kT[:, hh, :], fm[:, hh, 1, :], ALU.mult, [kT, fm], [ki])
                pe_ = self.ps()
                self.mm(pe_[:, 0:256], self.triIm1[dr][:], lt[:], True, True, [self.triIm1[dr], lt], [pe_])
                ke = ker.next()
                self.act(ke[:], pe_[:, 0:256], AF.Exp, [pe_], [ke], scale=1.0 / 16)
                self.tt(ke[:], ke[:], tm[:, 0:256], ALU.mult, [ke, tm], [ke])
                po_ = self.psb[7]
                for h in range(4):
                    hh, po = h // 2, 64 * (h % 2)
                    pa = self.ps()
                    self.mm(pa[:, 0:C], ki[po:po + 64, hh, :], qd[po:po + 64, hh, :], True, True, [ki, qd], [pa])
                    at = attr.next()
                    self.tt(at[:], pa[:, 0:C], self.triI[dr][:, 0:128], ALU.mult, [pa, self.triI[dr]], [at])
                    vh = tm[:, 256 + h * 128:256 + (h + 1) * 128]
                    self.mm(po_[:, h * 128:(h + 1) * 128], at[:], vh, True, False, [at, tm], [po_])
                    self.mm(po_[:, h * 128:(h + 1) * 128], qd[po:po + 64, hh, :],
                            S[po:po + 64, hh, (h % 2) * 128:(h % 2 + 1) * 128], False, True, [qd, S], [po_])
                for hh in range(2):
                    pd = self.ps()
                    self.mm(pd[:, 0:256], ke[:, hh * 128:(hh + 1) * 128], tm[:, 256 + hh * 256:256 + (hh + 1) * 256],
                            True, True, [ke, tm], [pd])
                    for j in range(2):
                        po = 64 * j
                        self.stt(S[po:po + 64, hh, j * 128:(j + 1) * 128], S[po:po + 64, hh, j * 128:(j + 1) * 128],
                                 dec[po:po + 64, hh:hh + 1], pd[po:po + 64, j * 128:(j + 1) * 128],
                                 ALU.mult, ALU.add, [S, dec, pd], [S])
                os_ = osr.next()
                if dr == 0:
                    self.copy(os_[:], po_[:], [po_], [os_], eng="act")
                    self.store("oF", d["oF"].ap()[0, c0:c0 + C, :], os_[:], os_)
                else:
                    of = ofr.next()
                    self.load(of[:], of, "oF", d["oF"].ap()[0, c0:c0 + C, :])
                    self.tt(os_[:], po_[:], of[:], ALU.add, [po_, of], [os_])
                    self.finalize_rms_gate(os_, _Sl(tm, 768), gbc, 0, c0, fr)
        p.pop_scope()

    def neumann(self, X, Y, Z, xr, yr):
        self.tt(Z[:], Y[:], self.ident[:], ALU.add, [Y, self.ident], [Z])
        for k in range(6):
            px = self.ps()
            lastk = (k == 5)
            self.mm(px[:, 0:128], Y[:], X[:], True, True, [Y, X], [px])
            if not lastk:
                self.mm(px[:, 128:256], X[:], Y[:], True, True, [X, Y], [px])
            Xn = xr.next()
            self.copy(Xn[:], px[:, 0:128], [px], [Xn], eng="act")
            if not lastk:
                Yn = yr.next()
                self.copy(Yn[:], px[:, 128:256], [px], [Yn], eng="dve")
            pz = self.ps()
            self.mm(pz[:, 0:128], Xn[:], Z[:], True, True, [Xn, Z], [pz])
            self.tt(Z[:], Z[:], pz[:, 0:128], ALU.add, [Z, pz], [Z])
            X = Xn
            if not lastk:
                Y = Yn

    def phase_gdn(self, l):
        p, nc, d = self.p, self.nc, self.dram
        p.push_scope()
        gbc = p.sbuf("gbc", [128, 128], F32)
        self.load(gbc[:], gbc, "gdn_g", d["gdn_g"].ap()[l])
        ab = p.sbuf("ab", [128, 16], F32)
        self.load(ab[:], ab, "gdn_ab", d["gdn_ab"].ap()[l])
        negA = p.sbuf("negA", [128, 8], F32)
        self.act(negA[:], ab[:, 0:8], AF.Exp, [ab], [negA])
        self.ts(negA[:], negA[:], -1.0, None, ALU.mult, None, [negA], [negA])
        qkvr = p.ring("qkvT", [128, 12, C], F32, 2)
        tmr = p.ring("tm", [128, 528], F32, 2)
        ktmr = p.ring("ktm", [128, 512], F32, 2)
        vtmr = p.ring("vtm", [128, 512], F32, 2)
        sm = p.ring("sm", [128, 48], F32, 2)
        LUr = p.ring("LU", [128, 4, C], F32, 2)
        Er = p.ring("E", [128, C], F32, 2)
        Xr = p.ring("X", [128, C], F32, 3)
        Yr = p.ring("Y", [128, C], F32, 3)
        Zr = p.ring("Z", [128, C], F32, 2)
        aqr = p.ring("aq", [128, C], F32, 2)
        vbr = p.ring("vb", [128, 256], F32, 2)
        U0r = p.ring("U0", [128, C], F32, 2)
        WTr = p.ring("WT", [128, C], F32, 2)
        ker = p.ring("ke", [128, C], F32, 2)
        vnr = p.ring("vn", [128, C], F32, 2)
        t2r = p.ring("t2", [128, C], F32, 2)
        osr = p.ring("os", [128, 512], F32, 2)
        ofr = p.ring("of", [128, 512], F32, 2)
        edr = p.ring("ed", [128, 4], F32, 2)
        fr = self.fin_rings()
        S = p.sbuf("S", [128, 4, 128], F32)
        for dr in (0, 1):
            self.memset(S, S[:], 0.0)
            for (seg, c0, first, lastc) in self.chunk_order(dr):
                qkv, tm = qkvr.next(), tmr.next()
                self.load(qkv[:], qkv, "pF", d["pF"].ap()[512:2048, c0:c0 + C].rearrange("(g p) t -> p g t", p=128))
                self.load(tm[:], tm, "pM", d["pM"].ap()[c0:c0 + C, 1280:1808])
                ktm, vtm = ktmr.next(), vtmr.next()
                pk, pv = self.ps(), self.ps()
                for h in range(4):
                    self.tr(pk[:, h * 128:(h + 1) * 128], qkv[:, 4 + h, :], [qkv], [pk])
                    self.tr(pv[:, h * 128:(h + 1) * 128], qkv[:, 8 + h, :], [qkv], [pv])
                self.copy(ktm[:], pk[:], [pk], [ktm], eng="act")
                self.copy(vtm[:], pv[:], [pv], [vtm], eng="dve")
                s = sm.next()
                self.tt(s[:, 0:4], tm[:, 512 + dr * 4:516 + dr * 4], ab[:, 8 + dr * 4:12 + dr * 4], ALU.add, [tm, ab], [s])
                self.ts(s[:, 32:36], s[:, 0:4], -1.0, None, ALU.mult, None, [s], [s])
                self.tt(s[:, 32:36], s[:, 32:36], s[:, 0:4], ALU.min, [s], [s])
                self.act(s[:, 32:36], s[:, 32:36], AF.Exp, [s], [s])
                self.ts(s[:, 36:40], s[:, 32:36], 2.0, None, ALU.add, None, [s], [s])
                self.recip(s[:, 36:40], s[:, 36:40], [s], [s])
                self.tt(s[:, 32:36], s[:, 32:36], s[:, 36:40], ALU.mult, [s], [s])
                self.tt(s[:, 36:40], s[:, 32:36], s[:, 32:36], ALU.mult, [s], [s])
                self.ts(s[:, 40:44], s[:, 36:40], 1.0 / 11, 1.0 / 9, ALU.mult, ALU.add, [s], [s])
                for cst in (1.0 / 7, 1.0 / 5, 1.0 / 3, 1.0):
                    self.tt(s[:, 40:44], s[:, 40:44], s[:, 36:40], ALU.mult, [s], [s])
                    self.ts(s[:, 40:44], s[:, 40:44], cst, None, ALU.add, None, [s], [s])
                self.stt(s[:, 40:44], s[:, 32:36], 2.0, s[:, 40:44], ALU.mult, ALU.mult, [s], [s])
                self.stt(s[:, 0:4], s[:, 0:4], 0.0, s[:, 40:44], ALU.max, ALU.add, [s], [s])
                self.tt(s[:, 0:4], s[:, 0:4], negA[:, dr * 4:dr * 4 + 4], ALU.mult, [s, negA], [s])
                self.act(s[:, 4:8], tm[:, 520 + dr * 4:524 + dr * 4], AF.Sigmoid, [tm], [s])
                self.ts(s[:, 8:12], s[:, 4:8], -1.0, None, ALU.mult, None, [s], [s])
                pc = self.ps()
                self.mm(pc[:, 0:4], self.triI[dr][:, 0:128], s[:, 0:4], True, True, [self.triI[dr], s], [pc])
                self.mm(pc[:, 4:8], self.triIm1[dr][:], s[:, 0:4], True, True, [self.triIm1[dr], s], [pc])
                self.mm(pc[:, 8:12], self.ones[:], s[:, 0:4], True, True, [self.ones, s], [pc])
                self.copy(s[:, 12:16], pc[:, 0:4], [pc], [s])
                self.ts(s[:, 16:20], pc[:, 0:4], -1.0, None, ALU.mult, None, [pc], [s])
                self.act(s[:, 28:32], pc[:, 0:4], AF.Exp, [pc], [s])
                self.ts(s[:, 20:24], s[:, 28:32], 128 ** -0.5, None, ALU.mult, None, [s], [s])
                self.tt(s[:, 28:32], s[:, 28:32], s[:, 4:8], ALU.mult, [s], [s])
                self.act(s[:, 24:28], pc[:, 4:8], AF.Exp, [pc], [s], scale=-1.0)
                ed = edr.next()
                self.act(ed[:], pc[:, 8:12], AF.Exp, [pc], [ed])
                LU = LUr.next()
                for h in range(4):
                    self.ts(LU[:, h, :], self.triI[dr][:, 0:128], s[:, h:h + 1], None, ALU.mult, None,
                            [self.triI[dr], s], [LU])
                os_ = osr.next()
                for h in range(4):
                    qT, kT = qkv[:, h, :], qkv[:, 4 + h, :]
                    kh, vh = ktm[:, h * 128:(h + 1) * 128], vtm[:, h * 128:(h + 1) * 128]
                    pa = self.ps()
                    self.mm(pa[:, 0:128], kT, kT, True, True, [qkv], [pa])
                    self.mm(pa[:, 128:256], self.nones[:], LU[:, h, :], True, False, [self.nones, LU], [pa])
                    self.mm(pa[:, 128:256], self.ident[:], self.negS[dr][:], False, True, [self.ident, self.negS[dr]], [pa])
                    self.mm(pa[:, 256:384], kT, qT, True, True, [qkv], [pa])
                    self.mm(pa[:, 384:512], self.ones[:], LU[:, h, :], True, False, [self.ones, LU], [pa])
                    self.mm(pa[:, 384:512], self.ident[:], self.negI[dr][:], False, True, [self.ident, self.negI[dr]], [pa])
                    E = Er.next()
                    self.act(E[:], pa[:, 128:256], AF.Exp, [pa, s], [E], bias=s[:, 12 + h:13 + h])
                    X = Xr.next()
                    self.stt(X[:], pa[:, 0:128], s[:, 8 + h:9 + h], E[:], ALU.mult, ALU.mult, [pa, s, E], [X])
                    E2 = Er.next()
                    self.act(E2[:], pa[:, 384:512], AF.Exp, [pa, s], [E2], bias=s[:, 16 + h:17 + h])
                    aq = aqr.next()
                    self.stt(aq[:], pa[:, 256:384], 128 ** -0.5, E2[:], ALU.mult, ALU.mult, [pa, E2], [aq])
                    pt = self.ps()
                    self.tr(pt[:, 0:128], X[:], [X], [pt])
                    Y = Yr.next()
                    self.copy(Y[:], pt[:, 0:128], [pt], [Y], eng="act")
                    self.dbg("gdn_s", s[:, 0:44], s, [128, 44])
                    self.dbg("gdn_X", X[:], X, [128, 128])
                    self.dbg("gdn_aq", aq[:], aq, [128, 128])
                    Z = Zr.next()
                    self.neumann(X, Y, Z, Xr, Yr)
                    self.dbg("gdn_Z", Z[:], Z, [128, 128])
                    vb = vbr.next()
                    self.ts(vb[:, 0:128], vh, s[:, 4 + h:5 + h], None, ALU.mult, None, [vtm, s], [vb])
                    self.ts(vb[:, 128:256], kh, s[:, 28 + h:29 + h], None, ALU.mult, None, [ktm, s], [vb])
                    pu = self.ps()
                    self.mm(pu[:, 0:128], Z[:], vb[:, 0:128], True, True, [Z, vb], [pu])
                    self.mm(pu[:, 128:256], vb[:, 128:256], Z[:], True, True, [Z, vb], [pu])
                    U0, WT = U0r.next(), WTr.next()
                    self.copy(U0[:], pu[:, 0:128], [pu], [U0], eng="act")
                    self.copy(WT[:], pu[:, 128:256], [pu], [WT], eng="dve")
                    self.dbg("gdn_U0", U0[:], U0, [128, 128])
                    self.dbg("gdn_ktm", ktm[:], ktm, [128, 512])
                    self.dbg("gdn_vtm", vtm[:], vtm, [128, 512])
                    ke = ker.next()
                    self.ts(ke[:], kh, s[:, 24 + h:25 + h], None, ALU.mult, None, [ktm, s], [ke])
                    pq = self.ps()
                    self.mm(pq[:, 0:128], WT[:], S[:, h, :], True, True, [WT, S], [pq])
                    self.mm(pq[:, 128:256], qT, S[:, h, :], True, True, [qkv, S], [pq])
                    vn = vnr.next()
                    self.tt(vn[:], U0[:], pq[:, 0:128], ALU.subtract, [U0, pq], [vn])
                    self.mm(pq[:, 256:384], aq[:], vn[:], True, True, [aq, vn], [pq])
                    self.mm(pq[:, 384:512], ke[:], vn[:], True, True, [ke, vn], [pq])
                    t2 = t2r.next()
                    self.copy(t2[:], pq[:, 256:384], [pq], [t2], eng="act")
                    if dr == 0:
                        self.stt(os_[:, h * 128:(h + 1) * 128], pq[:, 128:256], s[:, 20 + h:21 + h], t2[:],
                                 ALU.mult, ALU.add, [pq, s, t2], [os_])
                    else:
                        self.stt(os_[:, h * 128:(h + 1) * 128], pq[:, 128:256], s[:, 20 + h:21 + h], t2[:],
                                 ALU.mult, ALU.add, [pq, s, t2], [os_])
                    self.stt(S[:, h, :], S[:, h, :], ed[:, h:h + 1], pq[:, 384:512], ALU.mult, ALU.add,
                             [S, ed, pq], [S])
                if dr == 0:
                    self.store("oF", d["oF"].ap()[1, c0:c0 + C, :], os_[:], os_)
                else:
                    of = ofr.next()
                    self.load(of[:], of, "oF", d["oF"].ap()[1, c0:c0 + C, :])
                    self.tt(os_[:], os_[:], of[:], ALU.add, [os_, of], [os_])
                    self.finalize_rms_gate(os_, _Sl(tm, 0), gbc, 512, c0, fr)
        p.pop_scope()

    def phase_rwkv(self, l):
        p, nc, d = self.p, self.nc, self.dram
        p.push_scope()
        vec = p.sbuf("vec", [128, 20], F32)
        self.load(vec[:], vec, "rw_vec", d["rw_vec"].ap()[l])
        omka = p.sbuf("omka", [128, 4], F32)
        self.ts(omka[:], vec[:, 4:8], -1.0, 1.0, ALU.mult, ALU.add, [vec], [omka])
        g2 = p.sbuf("g2", [96, 512], F32)
        self.load(g2[:], g2, "rw_g2", d["rw_g2"].ap()[l])
        mu = p.sbuf("mu", [128, 2, 14], F32)
        self.load(mu[:], mu, "rw_mu", d["rw_mu"].ap()[l].rearrange("d p n -> p d n"))
        w0a0 = p.sbuf("w0a0", [128, 2, 8], F32)
        self.load(w0a0[:], w0a0, "rw_w0a0", d["rw_w0a0"].ap()[l].rearrange("d p n -> p d n"))
        w2a2 = p.sbuf("w2a2", [32, 2, 1024], F32)
        self.load(w2a2[:], w2a2, "rw_w2a2", d["rw_w2a2"].ap()[l].rearrange("d p n -> p d n"))
        pr = p.ring("praw", [128, 14, C + 2], F32, 2)
        xsr = p.ring("xs", [128, 14, C], F32, 2)
        gr = p.ring("gT", [96, C], F32, 2)
        tnr = p.ring("tn", [32, C], F32, 2)
        lwr = p.ring("lw", [128, 4, C], F32, 2)
        ar = p.ring("a", [128, 4, C], F32, 2)
        kkr = p.ring("kk", [128, 4, C], F32, 2)
        kpr = p.ring("kp", [128, 4, C], F32, 2)
        btr = p.ring("bt", [128, 4, C], F32, 2)
        tfr = p.ring("tf", [128, C], F32, 3)
        bnr = p.ring("bn", [128, 4, C], F32, 2)
        lwtr = p.ring("lwt", [128, 512], F32, 2)
        fcr = p.ring("fc", [128, 4, C], F32, 4)
        ecr = p.ring("ec", [128, 3, C], F32, 2)
        decr = p.ring("dec", [128, 4], F32, 2)
        eer = p.ring("ee", [128, 512], F32, 2)
        tmr = p.ring("tmq", [128, 512], F32, 8)
        Xr = p.ring("X", [128, C], F32, 3)
        Yr = p.ring("Y", [128, C], F32, 3)
        Zr = p.ring("Z", [128, C], F32, 2)
        mr = p.ring("m", [128, C], F32, 8)
        r1r = p.ring("r1", [128, 64], F32, 2)
        u0r = p.ring("u0", [128, 512], F32, 2)
        wkr = p.ring("wk", [128, 4, C], F32, 2)
        ur = p.ring("u", [128, 128], F32, 2)
        osr = p.ring("os", [128, 512], F32, 2)
        ofr = p.ring("of", [128, 512], F32, 2)
        str_ = p.ring("st", [128, 16], F32, 2)
        ynr = p.ring("yn", [128, 512], F32, 2)
        bfr = p.ring("bf", [128, 4, C], F32, 2)
        ybr = p.ring("yb", [128, 4, C], BF16, 2)
        gsr = p.ring("gs", [96, C], F32, 2)
        H = p.sbuf("H", [128, 4, 64], F32)
        for dr in (0, 1):
            self.memset(H, H[:], 0.0)
            sh = 0 if dr == 0 else 2
            for (seg, c0, first, lastc) in self.chunk_order(dr):
                seg0 = self.chunks[seg][0]
                segn = seg0 + len(self.chunks[seg]) * C
                praw = pr.next()
                lo, hi = max(c0 - 1, seg0), min(c0 + C + 1, segn)
                if lo > c0 - 1:
                    self.memset(praw, praw[:, :, 0:1], 0.0)
                if hi < c0 + C + 1:
                    self.memset(praw, praw[:, :, C + 1:C + 2], 0.0)
                a0, a1 = lo - (c0 - 1), hi - (c0 - 1)
                self.load(praw[:, 0:12, a0:a1], praw, "pF",
                          d["pF"].ap()[28 * 128:40 * 128, lo:hi].rearrange("(g p) t -> p g t", p=128))
                self.load(praw[0:32, 12:14, a0:a1], praw, "pF",
                          d["pF"].ap()[40 * 128:40 * 128 + 64, lo:hi].rearrange("(g p) t -> p g t", p=32))
                xs = xsr.next()
                for g in range(14):
                    pp = 128 if g < 12 else 32
                    tf = tfr.next()
                    self.tt(tf[0:pp, :], praw[0:pp, g, sh:sh + C], praw[0:pp, g, 1:1 + C], ALU.subtract, [praw], [tf])
                    self.stt(xs[0:pp, g, :], tf[0:pp, :], mu[0:pp, dr, g:g + 1], praw[0:pp, g, 1:1 + C],
                             ALU.mult, ALU.add, [tf, mu, praw], [xs])
                tn = tnr.next()
                self.act(tn[:, :], xs[0:32, 12, :], AF.Tanh, [xs], [tn])
                lw, a_ = lwr.next(), ar.next()
                for g in range(4):
                    pw = self.ps()
                    self.mm(pw[:, 0:C], w2a2[:, dr, g * 128:(g + 1) * 128], tn[:, :], True, True, [w2a2, tn], [pw])
                    self.mm(pw[:, C:2 * C], w2a2[:, dr, 512 + g * 128:512 + (g + 1) * 128], xs[0:32, 13, :], True, True,
                            [w2a2, xs], [pw])
                    self.act(lw[:, g, :], pw[:, 0:C], AF.Sigmoid, [pw, w0a0], [lw], bias=w0a0[:, dr, g:g + 1])
                    self.act(a_[:, g, :], pw[:, C:2 * C], AF.Sigmoid, [pw, w0a0], [a_], bias=w0a0[:, dr, 4 + g:5 + g])
                kk, kp, bt, bn = kkr.next(), kpr.next(), btr.next(), bnr.next()
                for g in range(4):
                    rT, kT, vT = xs[:, g, :], xs[:, 4 + g, :], xs[:, 8 + g, :]
                    self.ts(kk[:, g, :], kT, vec[:, g:g + 1], None, ALU.mult, None, [xs, vec], [kk])
                    tf = tfr.next()
                    self.act(tf[:], kk[:, g, :], AF.Square, [kk], [tf])
                    pn = self.ps()
                    self.mm(pn[:, 0:C], self.bones[:], tf[:], True, True, [self.bones, tf], [pn])
                    tf2 = tfr.next()
                    self.act(tf2[:], pn[:, 0:C], AF.Sqrt, [pn], [tf2], bias=self.eps_t[:, 0:1])
                    self.recip(tf2[:], tf2[:], [tf2], [tf2])
                    self.tt(kk[:, g, :], kk[:, g, :], tf2[:], ALU.mult, [kk, tf2], [kk])
                    tf3 = tfr.next()
                    self.ts(tf3[:], a_[:, g, :], vec[:, 4 + g:5 + g], omka[:, g:g + 1], ALU.mult, ALU.add,
                            [a_, vec, omka], [tf3])
                    self.tt(kp[:, g, :], kT, tf3[:], ALU.mult, [xs, tf3], [kp])
                    self.stt(bt[:, g, :], a_[:, g, :], -1.0, kk[:, g, :], ALU.mult, ALU.mult, [a_, kk], [bt])
                    tf4 = tfr.next()
                    self.stt(tf4[:], rT, vec[:, 8 + g:9 + g], kp[:, g, :], ALU.mult, ALU.mult, [xs, vec, kp], [tf4])
                    pb = self.ps()
                    self.mm(pb[:, 0:C], self.bones[:], tf4[:], True, True, [self.bones, tf4], [pb])
                    self.tt(bn[:, g, :], pb[:, 0:C], vT, ALU.mult, [pb, xs], [bn])
                if dr == 0:
                    self.store("bnF", d["bnF"].ap()[:, c0:c0 + C].rearrange("(g p) t -> p g t", p=128), bn[:], bn)
                lwt = lwtr.next()
                pl = self.ps()
                for g in range(4):
                    self.tr(pl[:, g * 128:(g + 1) * 128], lw[:, g, :], [lw], [pl])
                self.copy(lwt[:], pl[:], [pl], [lwt], eng="act")
                pe_ = self.ps()
                self.mm(pe_[:, :], self.triIm1[dr][:], lwt[:], True, True, [self.triIm1[dr], lwt], [pe_])
                ee = eer.next()
                self.act(ee[:], pe_[:], AF.Exp, [pe_], [ee], scale=RW_C)
                dec = decr.next()
                RP, KX, BI, KI = fcr.next(), fcr.next(), fcr.next(), fcr.next()
                for g in range(4):
                    pc = self.ps()
                    self.mm(pc[:, 0:129], lwt[:, g * 128:(g + 1) * 128], self.triI[dr][:], True, True,
                            [lwt, self.triI[dr]], [pc])
                    self.mm(pc[:, 256:384], lwt[:, g * 128:(g + 1) * 128], self.triS[dr][:], True, True,
                            [lwt, self.triS[dr]], [pc])
                    ec = ecr.next()
                    self.act(ec[:, 0, :], pc[:, 0:128], AF.Exp, [pc], [ec], scale=-RW_C)
                    self.act(ec[:, 1, :], pc[:, 256:384], AF.Exp, [pc], [ec], scale=-RW_C)
                    self.act(ec[:, 2, :], pc[:, 0:128], AF.Exp, [pc], [ec], scale=RW_C)
                    self.act(dec[:, g:g + 1], pc[:, 128:129], AF.Exp, [pc], [dec], scale=-RW_C)
                    self.tt(RP[:, g, :], xs[:, g, :], ec[:, 0, :], ALU.mult, [xs, ec], [RP])
                    self.tt(KX[:, g, :], kk[:, g, :], ec[:, 1, :], ALU.mult, [kk, ec], [KX])
                    self.tt(BI[:, g, :], bt[:, g, :], ec[:, 2, :], ALU.mult, [bt, ec], [BI])
                    self.tt(KI[:, g, :], kp[:, g, :], ec[:, 2, :], ALU.mult, [kp, ec], [KI])
                BE, KE, VT, KXT = tmr.next(), tmr.next(), tmr.next(), tmr.next()
                for (src, si, dst, mul) in ((bt, None, BE, True), (kp, None, KE, True), (xs, 8, VT, False),
                                            (KX, None, KXT, False)):
                    pt = self.ps()
                    for g in range(4):
                        sap = src[:, g, :] if si is None else src[:, si + g, :]
                        self.tr(pt[:, g * 128:(g + 1) * 128], sap, [src], [pt])
                    if mul:
                        self.tt(dst[:], pt[:], ee[:], ALU.mult, [pt, ee], [dst])
                    else:
                        self.copy(dst[:], pt[:], [pt], [dst], eng="act")
                self.dbg("rw_xs", xs[:, 0:12, :], xs, [128, 12, C])
                self.dbg("rw_xw", xs[0:32, 12:14, :], xs, [32, 2, C])
                self.dbg("rw_lw", lw[:], lw, [128, 4, C])
                self.dbg("rw_a", a_[:], a_, [128, 4, C])
                self.dbg("rw_kk", kk[:], kk, [128, 4, C])
                self.dbg("rw_kp", kp[:], kp, [128, 4, C])
                self.dbg("rw_bt", bt[:], bt, [128, 4, C])
                self.dbg("rw_ee", ee[:], ee, [128, 512])
                self.dbg("rw_RP", RP[:], RP, [128, 4, C])
                self.dbg("rw_KX", KX[:], KX, [128, 4, C])
                self.dbg("rw_BI", BI[:], BI, [128, 4, C])
                self.dbg("rw_KI", KI[:], KI, [128, 4, C])
                self.dbg("rw_BE", BE[:], BE, [128, 512])
                self.dbg("rw_VT", VT[:], VT, [128, 512])
                u0 = u0r.next()
                wk = wkr.next()
                QbTs, QkTs, ZZ = [], [], []
                for hd in range(8):
                    g, po = hd // 2, 64 * (hd % 2)
                    kx, bi, ki, rp = (KX[po:po + 64, g, :], BI[po:po + 64, g, :], KI[po:po + 64, g, :],
                                      RP[po:po + 64, g, :])
                    pa = self.ps()
                    self.mm(pa[:, 0:128], kx, bi, True, True, [KX, BI], [pa])
                    self.mm(pa[:, 128:256], bi, kx, True, True, [KX, BI], [pa])
                    self.mm(pa[:, 256:384], ki, kx, True, True, [KX, KI], [pa])
                    pq = self.ps()
                    self.mm(pq[:, 0:128], bi, rp, True, True, [BI, RP], [pq])
                    self.mm(pq[:, 128:256], ki, rp, True, True, [KI, RP], [pq])
                    X, Y = Xr.next(), Yr.next()
                    self.tt(X[:], pa[:, 0:128], self.triS[1 - dr][:], ALU.mult, [pa, self.triS[1 - dr]], [X])
                    self.tt(Y[:], pa[:, 128:256], self.triS[dr][:], ALU.mult, [pa, self.triS[dr]], [Y])
                    AakT, QbT, QkT = mr.next(), mr.next(), mr.next()
                    self.tt(AakT[:], pa[:, 256:384], self.triS[dr][:], ALU.mult, [pa, self.triS[dr]], [AakT])
                    self.tt(QbT[:], pq[:, 0:128], self.triI[dr][:, 0:128], ALU.mult, [pq, self.triI[dr]], [QbT])
                    self.tt(QkT[:], pq[:, 128:256], self.triI[dr][:, 0:128], ALU.mult, [pq, self.triI[dr]], [QkT])
                    self.dbg("rw_X", X[:], X, [128, 128])
                    self.dbg("rw_Y", Y[:], Y, [128, 128])
                    self.dbg("rw_QkT", QkT[:], QkT, [128, 128])
                    Z = Zr.next()
                    self.neumann(X, Y, Z, Xr, Yr)
                    self.dbg("rw_Z", Z[:], Z, [128, 128])
                    vh = VT[:, hd * 64:(hd + 1) * 64]
                    pr1 = self.ps()
                    self.mm(pr1[:, 0:64], AakT[:], vh, True, True, [AakT, VT], [pr1])
                    r1 = r1r.next()
                    self.copy(r1[:], pr1[:, 0:64], [pr1], [r1], eng="act")
                    self.mm(pr1[:, 64:128], Z[:], r1[:], True, True, [Z, r1], [pr1])
                    self.copy(u0[:, hd * 64:(hd + 1) * 64], pr1[:, 64:128], [pr1], [u0], eng="act")
                    self.mm(pr1[:, 128:256], KXT[:, g * 128:(g + 1) * 128], Z[:], True, True, [KXT, Z], [pr1])
                    self.copy(wk[po:po + 64, g, :], pr1[po:po + 64, 128:256], [pr1], [wk], eng="dve")
                    QbTs.append(QbT)
                    QkTs.append(QkT)
                    Hh = H[po:po + 64, g, :]
                    ps_ = self.ps()
                    self.mm(ps_[:, 0:64], wk[po:po + 64, g, :], Hh, True, True, [wk, H], [ps_])
                    if hd % 2 == 0:
                        u = ur.next()
                    self.tt(u[:, (hd % 2) * 64:(hd % 2 + 1) * 64], u0[:, hd * 64:(hd + 1) * 64], ps_[:, 0:64], ALU.add,
                            [u0, ps_], [u])
                    uh = u[:, (hd % 2) * 64:(hd % 2 + 1) * 64]
                    if hd == 0:
                        py = self.psb[7]
                    self.mm(py[:, hd * 64:(hd + 1) * 64], rp, Hh, True, False, [RP, H], [py])
                    self.mm(py[:, hd * 64:(hd + 1) * 64], QbT[:], uh, False, False, [QbT, u], [py])
                    self.mm(py[:, hd * 64:(hd + 1) * 64], QkT[:], vh, False, True, [QkT, VT], [py])
                    if hd % 2 == 1:
                        ph = self.ps()
                        self.mm(ph[:, 0:128], BE[:, g * 128:(g + 1) * 128], u[:], True, False, [BE, u], [ph])
                        self.mm(ph[:, 0:128], KE[:, g * 128:(g + 1) * 128], VT[:, g * 128:(g + 1) * 128], False, True,
                                [KE, VT], [ph])
                        for j in range(2):
                            pj = 64 * j
                            self.stt(H[pj:pj + 64, g, :], H[pj:pj + 64, g, :], dec[pj:pj + 64, g:g + 1],
                                     ph[pj:pj + 64, j * 64:(j + 1) * 64], ALU.mult, ALU.add, [H, dec, ph], [H])
                os_ = osr.next()
                if dr == 0:
                    self.copy(os_[:], py[:], [py], [os_], eng="act")
                    self.store("oF", d["oF"].ap()[2, c0:c0 + C, :], os_[:], os_)
                else:
                    of = ofr.next()
                    self.load(of[:], of, "oF", d["oF"].ap()[2, c0:c0 + C, :])
                    self.tt(os_[:], py[:], of[:], ALU.add, [py, of], [os_])
                    st = str_.next()
                    nc_ = self.nc
                    ov = os_[:].rearrange("p (h c) -> p h c", c=64)
                    p.op("dve", lambda: nc_.vector.tensor_reduce(out=st[:, 0:8], in_=ov, axis=AX.X, op=ALU.add),
                         reads=[os_], writes=[st])
                    self.ts(st[:, 0:8], st[:, 0:8], 1.0 / 64, None, ALU.mult, None, [st], [st])
                    yn = ynr.next()
                    for hd in range(8):
                        self.ts(yn[:, hd * 64:(hd + 1) * 64], os_[:, hd * 64:(hd + 1) * 64], st[:, hd:hd + 1], None,
                                ALU.subtract, None, [os_, st], [yn])
                    sq = ofr.next()
                    self.tt(sq[:], yn[:], yn[:], ALU.mult, [yn], [sq])
                    sv = sq[:].rearrange("p (h c) -> p h c", c=64)
                    p.op("dve", lambda: nc_.vector.tensor_reduce(out=st[:, 8:16], in_=sv, axis=AX.X, op=ALU.add),
                         reads=[sq], writes=[st])
                    self.act(st[:, 8:16], st[:, 8:16], AF.Sqrt, [st], [st], bias=self.gneps_t[:, 0:1], scale=1.0 / 64)
                    self.recip(st[:, 8:16], st[:, 8:16], [st], [st])
                    for hd in range(8):
                        self.ts(yn[:, hd * 64:(hd + 1) * 64], yn[:, hd * 64:(hd + 1) * 64], st[:, 8 + hd:9 + hd], None,
                                ALU.mult, None, [yn, st], [yn])
                    bf = bfr.next()
                    self.load(bf[:], bf, "bnF", d["bnF"].ap()[:, c0:c0 + C].rearrange("(g p) t -> p g t", p=128))
                    self.tt(bf[:], bf[:], bn[:], ALU.add, [bf, bn], [bf])
                    gT, gs = gr.next(), gsr.next()
                    self.load(gT[:], gT, "pF", d["pF"].ap()[41 * 128:41 * 128 + 96, c0:c0 + C])
                    self.act(gs[:], gT[:], AF.Sigmoid, [gT], [gs])
                    yb = ybr.next()
                    pt = self.ps()
                    pg = self.ps()
                    for g in range(4):
                        self.tr(pt[:, g * 128:(g + 1) * 128], yn[:, g * 128:(g + 1) * 128], [yn], [pt])
                        self.mm(pg[:, g * 128:(g + 1) * 128], g2[:, g * 128:(g + 1) * 128], gs[:], True, True,
                                [g2, gs], [pg])
                    for g in range(4):
                        tf = tfr.next()
                        self.ts(tf[:], pt[:, g * 128:(g + 1) * 128], vec[:, 12 + g:13 + g], vec[:, 16 + g:17 + g],
                                ALU.mult, ALU.add, [pt, vec], [tf])
                        self.tt(tf[:], tf[:], bf[:, g, :], ALU.add, [tf, bf], [tf])
                        self.tt(yb[:, g, :], tf[:], pg[:, g * 128:(g + 1) * 128], ALU.mult, [tf, pg], [yb])
                    self.store("yT", d["yT"].ap()[1536:2048, c0:c0 + C].rearrange("(g p) t -> p g t", p=128), yb[:], yb)
        p.pop_scope()

    def phase_p3(self, l, hname, oname):
        p, nc, d = self.p, self.nc, self.dram
        last = (l == self.L - 1)
        p.push_scope()
        yTr = p.ring("yTt", [128, KC, 512], BF16, 1)
        hTr = p.ring("hTt", [128, KC, 512], F32, 1)
        n2 = p.sbuf("n2", [128, KC, 512], BF16)
        actT = p.sbuf("actT", [128, FC // 2, 512], BF16)
        wr = p.ring("w3", [128, KC, 512], BF16, 3)
        wdr = p.ring("wd", [128, FC // 2, 128], BF16, 2)
        sqr = p.ring("sq", [128, 512], F32, 2)
        tmpr = p.ring("ntmp", [128, 512], F32, 2)
        rstd = p.sbuf("rstd", [128, 512], F32)
        gsl = p.ring("gsl", [128, 512], F32, 2)
        osb = p.ring("osb", [128, 512], F32, 2)
        mod = self.modT[l]
        for (t0, n, seg) in self.blocks:
            if last and seg == 1:
                continue
            yT, hT = yTr.next(), hTr.next()
            self.load(yT[:, :, :n], yT, "yT", d["yT"].ap()[:, t0:t0 + n].rearrange("(kc p) t -> p kc t", p=128))
            self.load(hT[:, :, :n], hT, hname, d[hname].ap()[:, t0:t0 + n].rearrange("(kc p) t -> p kc t", p=128))
            for ct in range(4):
                wt = wr.next()
                self.load(wt[:], wt, "w_out",
                          d["w_out"].ap()[l, :, ct * 512:(ct + 1) * 512].rearrange("(kc p) n -> p kc n", p=128), q="pool")
                for sub in range(4):
                    dc = ct * 4 + sub
                    pt = self.ps()
                    for kc in range(KC):
                        self.mm(pt[:, :n], wt[:, kc, sub * 128:(sub + 1) * 128], yT[:, kc, :n], kc == 0, kc == KC - 1,
                                [wt, yT], [pt])
                    self.stt(hT[:, dc, :n], pt[:, :n], mod[:, seg, 32 + dc:33 + dc], hT[:, dc, :n], ALU.mult, ALU.add,
                             [pt, mod, hT], [hT])
            self.modnorm(hT, n, self.G2[l], mod, 48, seg, n2, 0, sqr, rstd, tmpr)
            HF = FC // 2
            for half in range(2):
                for ft in range(HF // 4 + (1 if HF % 4 else 0)):
                    f0 = half * HF + ft * 4
                    nsub = min(4, half * HF + HF - f0)
                    wg, wu = wr.next(), wr.next()
                    self.load(wg[:, :, :nsub * 128], wg, "w_gu",
                              d["w_gu"].ap()[l, :, f0 * 128:(f0 + nsub) * 128].rearrange("(kc p) n -> p kc n", p=128),
                              q="pool")
                    self.load(wu[:, :, :nsub * 128], wu, "w_gu",
                              d["w_gu"].ap()[l, :, FFN + f0 * 128:FFN + (f0 + nsub) * 128].rearrange(
                                  "(kc p) n -> p kc n", p=128), q="pool")
                    for sub in range(nsub):
                        fc = f0 + sub - half * HF
                        pg, pu = self.ps(), self.ps()
                        for kc in range(KC):
                            self.mm(pg[:, :n], wg[:, kc, sub * 128:(sub + 1) * 128], n2[:, kc, :n], kc == 0,
                                    kc == KC - 1, [wg, n2], [pg])
                        for kc in range(KC):
                            self.mm(pu[:, :n], wu[:, kc, sub * 128:(sub + 1) * 128], n2[:, kc, :n], kc == 0,
                                    kc == KC - 1, [wu, n2], [pu])
                        gs = gsl.next()
                        self.act(gs[:, :n], pg[:, :n], AF.Silu, [pg], [gs])
                        self.tt(actT[:, fc, :n], gs[:, :n], pu[:, :n], ALU.mult, [gs, pu], [actT])
                for dc in range(KC):
                    wd = wdr.next()
                    self.load(wd[:], wd, "w_down",
                              d["w_down"].ap()[l, half * HF * 128:(half + 1) * HF * 128, dc * 128:(dc + 1) * 128].rearrange(
                                  "(fc p) n -> p fc n", p=128), q="pool")
                    pt = self.ps()
                    for fc in range(HF):
                        self.mm(pt[:, :n], wd[:, fc, :], actT[:, fc, :n], fc == 0, fc == HF - 1, [wd, actT], [pt])
                    self.stt(hT[:, dc, :n], pt[:, :n], mod[:, seg, 80 + dc:81 + dc], hT[:, dc, :n], ALU.mult, ALU.add,
                             [pt, mod, hT], [hT])
            if not last:
                self.store(oname, d[oname].ap()[:, t0:t0 + n].rearrange("(kc p) t -> p kc t", p=128), hT[:, :, :n], hT,
                           q="sp")
            else:
                self.rms_rstd(hT, n, sqr, rstd)
                for kc in range(KC):
                    ob = osb.next()
                    self.stt(ob[:, :n], hT[:, kc, :n], self.gfT[:, kc:kc + 1], rstd[:, :n], ALU.mult, ALU.mult,
                             [hT, self.gfT, rstd], [ob])
                    self.store("out", d["out"].ap()[kc * 128:(kc + 1) * 128, t0 - self.CTX:t0 - self.CTX + n], ob[:, :n],
                               ob, q="sp")
        p.pop_scope()

    def build(self, phases=None):
        p = self.p
        self.declare()
        self.consts()
        self.eps_t = p.sbuf("eps_t", [128, 1], F32)
        self.memset(self.eps_t, self.eps_t[:], EPS)
        self.gneps_t = p.sbuf("gneps_t", [128, 1], F32)
        self.memset(self.gneps_t, self.gneps_t[:], 64e-5)
        self.one_t = p.sbuf("one_t", [128, 1], F32)
        self.memset(self.one_t, self.one_t[:], 1.0)
        self.phase_mod()
        for l in range(self.L):
            hname = "xT" if l == 0 else "hA"
            ph = phases or ("p1", "gla", "gdn", "rwkv", "p3")
            if "p1" in ph:
                self.phase_p1(l, hname)
            if "gla" in ph:
                self.phase_gla(l)
            if "gdn" in ph:
                self.phase_gdn(l)
            if "rwkv" in ph:
                self.phase_rwkv(l)
            if "p3" in ph:
                self.phase_p3(l, hname, "hA")
        p.barrier()
        p.close()
        return self.nc


class _Sl:
    def __init__(self, t, off):
        self.t, self.off, self.res = t, off, t.res

    def __getitem__(self, idx):
        a, b = idx
        return self.t.th[a, slice(self.off + b.start, self.off + b.stop)]


_OFF = np.cumsum([0, 256, 256, 512, 512, 16, 1536, 512, 8, 8, 512, 512, 512, 1536, 64, 96])


def _fm(v, ngroups):
    return np.ascontiguousarray(np.swapaxes(v.reshape(v.shape[:-1] + (ngroups, 128)), -1, -2))


def prep_shared(inp, L):
    f = np.float32
    o = _OFF
    w_in = inp["w_in"]
    g_q, g_k, g_v, g_r, g_lo = (w_in[:, :, o[i]:o[i + 1]] for i in range(5))
    d_qkv, d_z, d_a, d_b = (w_in[:, :, o[i]:o[i + 1]] for i in range(5, 9))
    c_b, c_c, c_h = (w_in[:, :, o[i]:o[i + 1]] for i in range(9, 12))
    r_rkv, r_wa, r_g = (w_in[:, :, o[i]:o[i + 1]] for i in range(12, 15))
    z = lambda n: np.zeros((L, D, n), f)
    conv = []
    for g in range(4):
        sl = slice(g * 128, (g + 1) * 128)
        conv += [c_b[:, :, sl], c_c[:, :, sl], c_h[:, :, sl]]
    w_inF = np.concatenate([g_q, g_k, d_qkv] + conv + [r_rkv, r_wa, z(64), r_g, z(32), g_lo, z(112)], axis=2)
    w_inM = np.concatenate([g_k, g_v, g_r, d_z, d_a, d_b], axis=2)
    assert w_inF.shape[2] == NF and w_inM.shape[2] == NM
    sh = {}
    sh["ada_w"] = np.ascontiguousarray(inp["ada_w"], f)
    sh["ada_bT"] = _fm(inp["ada_b"], 96)
    sh["g1T"] = _fm(inp["norm1_g"], KC)
    sh["g2T"] = _fm(inp["norm2_g"], KC)
    sh["gfT"] = _fm(inp["final_g"], KC)
    sh["w_inF"] = np.ascontiguousarray(w_inF)
    sh["w_inM"] = np.ascontiguousarray(w_inM)
    sh["w_out"] = np.ascontiguousarray(inp["w_out"], f)
    sh["w_gu"] = np.ascontiguousarray(inp["ffn_w_gu"], f)
    sh["w_down"] = np.ascontiguousarray(inp["ffn_w_down"], f)
    sh["gla_aw"] = np.ascontiguousarray(np.concatenate([inp["gla_a_up"], inp["gla_a_b"][:, :, None, :]], axis=2))
    bc = lambda v: np.ascontiguousarray(np.broadcast_to(v[:, None, :], (L, 128, v.shape[-1])))
    sh["gla_g"] = bc(inp["gla_norm_g"])
    sh["gdn_g"] = bc(inp["gdn_norm_g"])
    gc = inp["gdn_conv"]
    sh["gdn_convT"] = np.ascontiguousarray(np.transpose(gc.reshape(L, 3, 12, 128), (0, 3, 2, 1)))
    scv = inp["sc_conv"]
    sh["sc_convT"] = np.ascontiguousarray(np.transpose(scv.reshape(L, 3, 4, 128), (0, 3, 2, 1)))
    sh["gdn_ab"] = bc(np.concatenate([inp["gdn_a_log"].reshape(L, 8), inp["gdn_dt_bias"].reshape(L, 8)], axis=1))
    mu_rkv = _fm(inp["rw_mu_rkv"], 12)
    mu_wa = np.zeros((L, 2, 128, 2), f)
    mu_wa[:, :, 0:32, 0] = inp["rw_mu_wa"][:, :, 0:32]
    mu_wa[:, :, 0:32, 1] = inp["rw_mu_wa"][:, :, 32:64]
    sh["rw_mu"] = np.ascontiguousarray(np.concatenate([mu_rkv, mu_wa], axis=3))
    sh["rw_w0a0"] = np.ascontiguousarray(np.concatenate([_fm(inp["rw_w0"], 4), _fm(inp["rw_a0"], 4)], axis=3))
    sh["rw_w2a2"] = np.ascontiguousarray(np.concatenate([inp["rw_w2"], inp["rw_a2"]], axis=3))
    sh["rw_g2"] = np.ascontiguousarray(inp["rw_g2"], f)
    sh["rw_vec"] = np.ascontiguousarray(np.concatenate(
        [_fm(inp["rw_kk"], 4), _fm(inp["rw_ka"], 4), _fm(inp["rw_rk"].reshape(L, 512), 4), _fm(inp["rw_gn_w"], 4),
         _fm(inp["rw_gn_b"], 4)], axis=2))
    return {k: np.asarray(v, f) for k, v in sh.items()}


def prep_core(inp, b):
    f = np.float32
    xT = np.ascontiguousarray(np.concatenate([inp["ctx"][b].T, inp["x"][b].T], axis=1), f)
    cc = np.stack([inp["c"][b], inp["c_ctx"]], axis=-1)
    cT = np.ascontiguousarray(np.swapaxes(cc.reshape(KC, 128, 2), 0, 1), f)
    return {"xT": xT, "cT": cT}


def kernel(**inputs):
    inp = {k: np.asarray(v) for k, v in inputs.items()}
    B, SEQ, _ = inp["x"].shape
    CTX = inp["ctx"].shape[1]
    L = inp["w_in"].shape[0]
    kb = K(SEQ, CTX, L)
    nc = kb.build()
    sh = prep_shared(inp, L)
    in_maps = []
    ncores = B
    for core in range(ncores):
        m = dict(sh)
        m.update(prep_core(inp, core % B))
        in_maps.append(m)
    res = run_bass_kernel_spmd(nc, in_maps, core_ids=list(range(ncores)))
    out = np.stack([np.ascontiguousarray(res.results[b]["out"].T) for b in range(B)], axis=0)
    return out.astype(np.float32)
```
